# Optimizing a Trainium2 kernel written in Bass

```python
import jax, jax.numpy as jnp
from jax import lax
import numpy as np

D_MODEL = 2048
BATCH = 2
SEQ = 4096
DEPTH = 2
DEC_BATCH = 2
DEC_SEQ = 16384
PAST_LEN = 128

N_MEM = 256
MLA_HEADS = 8
MLA_NOPE = 128
MLA_ROPE = 64
MLA_V = 128
Q_LORA = 512
KV_LORA = 256
HG_HEADS = 8
HG_DK = 128
HG_DV = 128
CHUNK = 64
MIX_WIDTH = MLA_HEADS * MLA_V + HG_HEADS * HG_DV
OFF_CQ = 0
OFF_CKV = OFF_CQ + Q_LORA
OFF_KR = OFF_CKV + KV_LORA
OFF_HQ = OFF_KR + MLA_ROPE
OFF_HFF = OFF_HQ + HG_HEADS * HG_DK
OFF_HFB = OFF_HFF + HG_HEADS * HG_DK
OFF_HI = OFF_HFB + HG_HEADS * HG_DK
OFF_HG = OFF_HI + HG_HEADS * HG_DV
IN_COLS = OFF_HG + HG_HEADS * HG_DV
XATTN_HEADS = 4
XATTN_DH = D_MODEL // XATTN_HEADS
D_FF = 4 * D_MODEL
Q_BLOCK = 128
ROPE_THETA = 10000.0
EPS = 1e-6

kernel_name = "hybrid_mla_hgrn2_bidir_encoder"


def rms_norm(x, g):
    xf = x.astype(jnp.float32)
    y = xf * lax.rsqrt(jnp.mean(xf * xf, axis=-1, keepdims=True) + EPS)
    return (y * g.astype(jnp.float32)).astype(x.dtype)


def apply_rope(x, seq_len):
    half = x.shape[-1] // 2
    pos = jnp.arange(seq_len, dtype=jnp.float32)
    inv_freq = ROPE_THETA ** (-jnp.arange(half, dtype=jnp.float32) / half)
    ang = pos[:, None] * inv_freq[None, :]
    cos = jnp.cos(ang)[None, :, None, :]
    sin = jnp.sin(ang)[None, :, None, :]
    xf = x.astype(jnp.float32)
    x1, x2 = xf[..., :half], xf[..., half:]
    out = jnp.concatenate([x1 * cos - x2 * sin, x2 * cos + x1 * sin], axis=-1)
    return out.astype(x.dtype)


def block_attention(q, k, v, scale):
    b, s, h, dq = q.shape
    nb = s // Q_BLOCK
    qb = q.reshape(b, nb, Q_BLOCK, h, dq).transpose(1, 0, 2, 3, 4)

    def one_block(qblk):
        sc = jnp.einsum('bqhd,bkhd->bhqk', qblk, k).astype(jnp.float32) * scale
        p = jax.nn.softmax(sc, axis=-1).astype(v.dtype)
        return jnp.einsum('bhqk,bkhd->bqhd', p, v)

    o = lax.map(one_block, qb)
    return o.transpose(1, 0, 2, 3, 4).reshape(b, s, h, v.shape[-1])


def mla_group(proj, q_norm, w_uq, kv_norm, w_ukv):
    b, s, _ = proj.shape
    c_q = proj[..., OFF_CQ:OFF_CQ + Q_LORA]
    c_kv = proj[..., OFF_CKV:OFF_CKV + KV_LORA]
    k_r = proj[..., OFF_KR:OFF_KR + MLA_ROPE]
    q = (rms_norm(c_q, q_norm) @ w_uq).reshape(b, s, MLA_HEADS, MLA_NOPE + MLA_ROPE)
    kv = (rms_norm(c_kv, kv_norm) @ w_ukv).reshape(b, s, MLA_HEADS, MLA_NOPE + MLA_V)
    q_nope, q_rope = q[..., :MLA_NOPE], q[..., MLA_NOPE:]
    k_nope, v = kv[..., :MLA_NOPE], kv[..., MLA_NOPE:]
    q_rope = apply_rope(q_rope, s)
    k_rope = apply_rope(k_r[:, :, None, :], s)
    q = jnp.concatenate([q_nope, q_rope], axis=-1)
    k = jnp.concatenate([k_nope, jnp.broadcast_to(k_rope, (b, s, MLA_HEADS, MLA_ROPE))], axis=-1)
    o = block_attention(q, k, v, (MLA_NOPE + MLA_ROPE) ** -0.5)
    return o.reshape(b, s, MLA_HEADS * MLA_V)


def hgrn_direction(q, k, v, logf):
    b, s, h, dk = q.shape
    dv = v.shape[-1]
    n = s // CHUNK

    def chunks(t):
        return t.reshape(b, n, CHUNK, h, t.shape[-1]).transpose(1, 0, 3, 2, 4)

    lower = jnp.tril(jnp.ones((CHUNK, CHUNK), dtype=bool))[None, None, :, :, None]

    def step(state, inp):
        qc, kc, vc, lc = inp
        bcum = jnp.cumsum(lc, axis=2)
        inter = jnp.einsum('bhtk,bhkv->bhtv', qc * jnp.exp(bcum), state)
        rel = bcum[:, :, :, None, :] - bcum[:, :, None, :, :]
        decay = jnp.exp(jnp.where(lower, rel, -jnp.inf))
        scores = jnp.einsum('bhtk,bhsk,bhtsk->bhts', qc, kc, decay)
        intra = jnp.einsum('bhts,bhsv->bhtv', scores, vc)
        btot = bcum[:, :, -1:, :]
        new_state = jnp.exp(btot[:, :, 0, :])[..., None] * state + jnp.einsum(
            'bhsk,bhsv->bhkv', kc * jnp.exp(btot - bcum), vc)
        return new_state, inter + intra

    init = jnp.zeros((b, h, dk, dv), jnp.float32)
    _, o = lax.scan(step, init, (chunks(q), chunks(k), chunks(v), chunks(logf)))
    return o.transpose(1, 0, 3, 2, 4).reshape(b, s, h, dv)


def hgrn2_group(proj, lb, out_norm):
    b, s, _ = proj.shape
    f32 = jnp.float32
    q = jax.nn.silu(proj[..., OFF_HQ:OFF_HFF].astype(f32)) * (HG_DK ** -0.5)
    q = q.reshape(b, s, HG_HEADS, HG_DK)
    v = proj[..., OFF_HI:OFF_HG].astype(f32).reshape(b, s, HG_HEADS, HG_DV)
    g = proj[..., OFF_HG:IN_COLS].astype(f32).reshape(b, s, HG_HEADS, HG_DV)

    def gates(z, lbd):
        z = z.astype(f32).reshape(b, s, HG_HEADS, HG_DK)
        lbd = lbd.astype(f32).reshape(HG_HEADS, HG_DK)
        logf = jnp.logaddexp(jnp.log(lbd), jnp.log1p(-lbd) + jax.nn.log_sigmoid(z))
        k = (1.0 - lbd) * jax.nn.sigmoid(-z)
        return k, logf

    k_f, lf_f = gates(proj[..., OFF_HFF:OFF_HFB], lb[0])
    k_b, lf_b = gates(proj[..., OFF_HFB:OFF_HI], lb[1])
    o_f = hgrn_direction(q, k_f, v, lf_f)
    o_b = jnp.flip(hgrn_direction(jnp.flip(q, 1), jnp.flip(k_b, 1), jnp.flip(v, 1), jnp.flip(lf_b, 1)), 1)
    o = o_f + o_b
    o = o * lax.rsqrt(jnp.mean(o * o, axis=-1, keepdims=True) + EPS) * out_norm.astype(f32)
    o = o * jax.nn.silu(g)
    return o.reshape(b, s, HG_HEADS * HG_DV).astype(proj.dtype)


def memory_cross_attention(h, m, w_xq, w_xk, w_xv, w_xo):
    b, s, _ = h.shape
    nm = m.shape[1]
    q = (h @ w_xq).reshape(b, s, XATTN_HEADS, XATTN_DH)
    k = (m @ w_xk).reshape(b, nm, XATTN_HEADS, XATTN_DH)
    v = (m @ w_xv).reshape(b, nm, XATTN_HEADS, XATTN_DH)
    sc = jnp.einsum('bqhd,bkhd->bhqk', q, k).astype(jnp.float32) * (XATTN_DH ** -0.5)
    p = jax.nn.softmax(sc, axis=-1).astype(v.dtype)
    o = jnp.einsum('bhqk,bkhd->bqhd', p, v).reshape(b, s, D_MODEL)
    return o @ w_xo


def trunk(x, mem, mix_norm, w_in, q_norm, w_uq, kv_norm, w_ukv, hgrn_lb_logits, hgrn_out_norm,
          w_out, xattn_norm, mem_norm, w_xq, w_xk, w_xv, w_xo, ffn_norm, w_ffn1, w_ffn2, final_norm):
    lb_all = jnp.cumsum(jax.nn.softmax(hgrn_lb_logits.astype(jnp.float32), axis=0), axis=0)
    lb_all = lb_all - lb_all[0:1]
    for l in range(DEPTH):
        h = rms_norm(x, mix_norm[l])
        proj = h @ w_in[l]
        a_out = mla_group(proj, q_norm[l], w_uq[l], kv_norm[l], w_ukv[l])
        b_out = hgrn2_group(proj, lb_all[l], hgrn_out_norm[l])
        x = x + jnp.concatenate([a_out, b_out], axis=-1) @ w_out[l]
        h = rms_norm(x, xattn_norm[l])
        m = rms_norm(mem, mem_norm[l])
        x = x + memory_cross_attention(h, m, w_xq[l], w_xk[l], w_xv[l], w_xo[l])
        h = rms_norm(x, ffn_norm[l])
        u = jax.nn.relu(h @ w_ffn1[l])
        x = x + (u * u) @ w_ffn2[l]
    return rms_norm(x, final_norm)


def setup_inputs(seed: int = 0) -> dict:
    key = jax.random.key(seed)
    ks = jax.random.split(key, 24)
    f32 = jnp.float32

    def w(k, shape, fan_in):
        return jax.random.normal(k, shape, f32) * (fan_in ** -0.5)

    def gain(k, shape):
        return 1.0 + 0.01 * jax.random.normal(k, shape, f32)

    return {
        'x_prompt': jax.random.normal(ks[0], (BATCH, SEQ, D_MODEL), f32),
        'x_sample': jax.random.normal(ks[1], (DEC_BATCH, DEC_SEQ, D_MODEL), f32),
        'mem_prompt': jax.random.normal(ks[2], (BATCH, N_MEM, D_MODEL), f32),
        'mem_sample': jax.random.normal(ks[3], (DEC_BATCH, N_MEM, D_MODEL), f32),
        'mix_norm': gain(ks[4], (DEPTH, D_MODEL)),
        'w_in': w(ks[5], (DEPTH, D_MODEL, IN_COLS), D_MODEL),
        'q_norm': gain(ks[6], (DEPTH, Q_LORA)),
        'w_uq': w(ks[7], (DEPTH, Q_LORA, MLA_HEADS * (MLA_NOPE + MLA_ROPE)), Q_LORA),
        'kv_norm': gain(ks[8], (DEPTH, KV_LORA)),
        'w_ukv': w(ks[9], (DEPTH, KV_LORA, MLA_HEADS * (MLA_NOPE + MLA_V)), KV_LORA),
        'hgrn_lb_logits': 0.5 * jax.random.normal(ks[10], (DEPTH, 2, HG_HEADS * HG_DK), f32),
        'hgrn_out_norm': gain(ks[11], (DEPTH, HG_DV)),
        'w_out': w(ks[12], (DEPTH, MIX_WIDTH, D_MODEL), MIX_WIDTH),
        'xattn_norm': gain(ks[13], (DEPTH, D_MODEL)),
        'mem_norm': gain(ks[14], (DEPTH, D_MODEL)),
        'w_xq': w(ks[15], (DEPTH, D_MODEL, D_MODEL), D_MODEL),
        'w_xk': w(ks[16], (DEPTH, D_MODEL, D_MODEL), D_MODEL),
        'w_xv': w(ks[17], (DEPTH, D_MODEL, D_MODEL), D_MODEL),
        'w_xo': w(ks[18], (DEPTH, D_MODEL, D_MODEL), D_MODEL),
        'ffn_norm': gain(ks[19], (DEPTH, D_MODEL)),
        'w_ffn1': w(ks[20], (DEPTH, D_MODEL, D_FF), D_MODEL),
        'w_ffn2': w(ks[21], (DEPTH, D_FF, D_MODEL), D_FF),
        'final_norm': gain(ks[22], (D_MODEL,)),
    }


def reference(x_prompt, x_sample, mem_prompt, mem_sample, mix_norm, w_in, q_norm, w_uq, kv_norm,
              w_ukv, hgrn_lb_logits, hgrn_out_norm, w_out, xattn_norm, mem_norm, w_xq, w_xk, w_xv,
              w_xo, ffn_norm, w_ffn1, w_ffn2, final_norm):
    y_prompt = trunk(x_prompt, mem_prompt, mix_norm, w_in, q_norm, w_uq, kv_norm, w_ukv,
                     hgrn_lb_logits, hgrn_out_norm, w_out, xattn_norm, mem_norm, w_xq, w_xk, w_xv,
                     w_xo, ffn_norm, w_ffn1, w_ffn2, final_norm)
    y_sample = trunk(x_sample, mem_sample, mix_norm, w_in, q_norm, w_uq, kv_norm, w_ukv,
                     hgrn_lb_logits, hgrn_out_norm, w_out, xattn_norm, mem_norm, w_xq, w_xk, w_xv,
                     w_xo, ffn_norm, w_ffn1, w_ffn2, final_norm)
    return (y_prompt, y_sample)
```

```python
import numpy as np
from contextlib import ExitStack
import concourse.bass as bass
import concourse.mybir as mybir
from concourse.bass_utils import run_bass_kernel_spmd

F32 = mybir.dt.float32
BF16 = mybir.dt.bfloat16
AF = mybir.ActivationFunctionType
ALU = mybir.AluOpType

D = 2048
KC = 16
NMEM = 256
H = 8
QL = 512
KVL = 256
ROPE = 64
DFF = 8192
IN_COLS = 5952
OFF_CQ, OFF_CKV, OFF_KR, OFF_HQ, OFF_HFF, OFF_HFB, OFF_HI, OFF_HG = 0, 512, 768, 832, 1856, 2880, 3904, 4928
EPS = 1e-6
TG = 1024
SB = 512
ATT_SCALE = 192.0 ** -0.5
LN_QS = float(np.log(128.0 ** -0.5))


class Buf:
    __slots__ = ("name", "w", "r", "sem")

    def __init__(self, name):
        self.name = name
        self.w = {}
        self.r = {}
        self.sem = None


class Prog:
    ENG = ("pe", "act", "dve", "pool", "sp")

    def __init__(self, nc, stack):
        self.nc = nc
        self.stack = stack
        self.eng = {"pe": nc.tensor, "act": nc.scalar, "dve": nc.vector, "pool": nc.gpsimd, "sp": nc.sync}
        self.sem = {}
        self.cnt = {}
        for e in self.ENG:
            self.sem[e] = stack.enter_context(nc.semaphore("S_" + e))
            self.cnt[e] = 0
        self.known = {e: {} for e in self.ENG}
        self.dcnt = {}
        self.dsem = {}
        self.ninstr = 0
        self.named = {}
        self.pool = []
        self.pool_idx = 0
        self.hist = {}
        self.esem_ids = None

    def newsem(self, name):
        if self.pool_idx == len(self.pool):
            s = self.stack.enter_context(self.nc.semaphore(f"DP{len(self.pool)}"))
            self.pool.append(s)
            self.dcnt[id(s)] = 0
            self.dsem[id(s)] = s
        s = self.pool[self.pool_idx]
        self.pool_idx += 1
        return s

    def reset_pool(self):
        self.pool_idx = 0

    def _waits(self, eng, reads, writes, self_ok=False):
        need = {}
        for b in reads:
            for k, v in b.w.items():
                if k not in need or need[k][1] < v[1]:
                    need[k] = v
        for b in writes:
            for k, v in b.w.items():
                if k not in need or need[k][1] < v[1]:
                    need[k] = v
            for k, v in b.r.items():
                if k not in need or need[k][1] < v[1]:
                    need[k] = v
        kn = self.known[eng]
        own = id(self.sem[eng]) if (eng == "pe" or self_ok) else None
        e = self.eng[eng]
        hist = self.hist
        for k, (s, c) in sorted(need.items(), key=lambda kv: -kv[1][1]):
            if k == own:
                continue
            if kn.get(k, 0) < c:
                kn[k] = c
                e.wait_ge(s, c)
                self.ninstr += 1
                snap = hist.get((k, c))
                if snap is not None:
                    for kk, cc in snap:
                        if kn.get(kk, 0) < cc:
                            kn[kk] = cc

    def _snap(self, eng, k, c):
        if self.esem_ids is None:
            self.esem_ids = [id(self.sem[e]) for e in self.ENG]
        kn = self.known[eng]
        self.hist[(k, c)] = tuple((kk, kn.get(kk, 0)) for kk in self.esem_ids if kn.get(kk, 0) > 0)

    def op(self, eng, fn, reads=(), writes=(), inc=True, self_ok=False):
        self._waits(eng, reads, writes, self_ok)
        s = self.sem[eng]
        seq = self.cnt[eng] + 1
        if inc:
            self.cnt[eng] = seq
        ev = (s, seq)
        k = id(s)
        if inc:
            self._snap(eng, k, seq)
        for b in reads:
            b.r[k] = ev
        for b in writes:
            b.w = {k: ev}
            b.r = {}
        ins = fn(self.eng[eng])
        self.ninstr += 1
        if inc:
            ins.then_inc(s, 1)

    def dma(self, eng, out, in_, reads, writes, sembuf):
        if sembuf.sem is None:
            sembuf.sem = self.newsem("D_" + sembuf.name)
        s = sembuf.sem
        k = id(s)
        self._waits(eng, reads, writes)
        self.dcnt[k] += 16
        ev = (s, self.dcnt[k])
        self._snap(eng, k, self.dcnt[k])
        for b in reads:
            b.r[k] = ev
        for b in writes:
            nw = {kk: vv for kk, vv in b.w.items() if kk in self.dcnt and kk != k}
            nw[k] = ev
            b.w = nw
            b.r = {}
        self.eng[eng].dma_start(out=out, in_=in_).then_inc(s, 16)
        self.ninstr += 1

    def barrier(self):
        for e in self.ENG:
            en = self.eng[e]
            kn = self.known[e]
            for e2 in self.ENG:
                if e2 == e or self.cnt[e2] == 0:
                    continue
                k = id(self.sem[e2])
                if kn.get(k, 0) < self.cnt[e2]:
                    kn[k] = self.cnt[e2]
                    en.wait_ge(self.sem[e2], self.cnt[e2])
            for k, c in self.dcnt.items():
                if c > 0 and kn.get(k, 0) < c:
                    kn[k] = c
                    en.wait_ge(self.dsem[k], c)


def build_program(SP, SS):
    nc = bass.Bass("TRN2", target_bir_lowering=False)
    SMAX = max(SP, SS)

    def din(name, shape, dt=F32):
        return nc.dram_tensor(name, list(shape), dt, kind="ExternalInput").ap()

    def dout(name, shape):
        return nc.dram_tensor(name, list(shape), F32, kind="ExternalOutput").ap()

    def dscr(name, shape, dt):
        return nc.dram_tensor(name, list(shape), dt).ap()

    x_in = {"p": din("x_p", [SP, D]), "s": din("x_s", [SS, D])}
    mem_in = {"p": din("mem_p", [NMEM, D]), "s": din("mem_s", [NMEM, D])}
    y_out = {"p": dout("y_p", [SP, D]), "s": dout("y_s", [SS, D])}
    w_in = din("w_in", [2, D, IN_COLS])
    w_uq = din("w_uq", [2, QL, 1536])
    w_ukv = din("w_ukv", [2, KVL, 2048])
    w_out = din("w_out", [2, D, D])
    w_xq = din("w_xq", [2, D, D])
    w_xk = din("w_xk", [2, D, D])
    w_xv = din("w_xv", [2, D, D])
    w_xo = din("w_xo", [2, D, D])
    w_f1 = din("w_ffn1", [2, D, DFF])
    w_f2 = din("w_ffn2", [2, DFF, D])
    prm_a = din("prm_a", [128, 128])
    prm_b = din("prm_b", [128, 128])
    fin_g = din("final_norm", [D])
    cos_t = din("cos_t", [64, SMAX])
    sin_t = din("sin_t", [64, SMAX])
    ident_d = din("ident", [128, 128])
    masks_d = din("masks", [128, 3, 512])

    xT = dscr("xT", [D, SMAX], F32)
    qT = dscr("qT", [1536, SMAX], BF16)
    latT = dscr("latT", [320, SMAX], BF16)
    hqT = dscr("hqT", [1024, SMAX], F32)
    lfT = [dscr("lfTf", [1024, SMAX], F32), dscr("lfTb", [1024, SMAX], F32)]
    kkT = [dscr("kkTf", [1024, SMAX], F32), dscr("kkTb", [1024, SMAX], F32)]
    sgT = dscr("sgT", [1024, SMAX], F32)
    vtok = dscr("vtok", [H, SMAX, 128], BF16)
    ofT = dscr("ofT", [1024, SMAX], F32)
    mixT = dscr("mixT", [D, SMAX], BF16)

    with ExitStack() as top:
        p = Prog(nc, top)

        uid = [0]

        def sb(stack, name, shape, dt=F32):
            uid[0] += 1
            return stack.enter_context(nc.sbuf_tensor(f"{name}_u{uid[0]}", list(shape), dt))

        ident = sb(top, "ident", [128, 128]); Bident = Buf("ident")
        identb = sb(top, "identb", [128, 128], BF16); Bidentb = Buf("identb")
        ones_b = sb(top, "ones_b", [128, 128], BF16); Bones = Buf("ones")
        ones_f = sb(top, "ones_f", [128, 128]); Bonesf = Buf("onesf")
        masks = sb(top, "masks", [128, 3, 512]); Bmasks = Buf("masks")
        PA = sb(top, "PA", [128, 128]); PB = sb(top, "PB", [128, 128]); Bprm = Buf("prm")
        LB = sb(top, "LB", [128, 2, 2, 8, 2]); BLB = Buf("LB")
        PS2 = [top.enter_context(nc.psum_tensor(f"ps2_{i}", [128, 1024], F32)) for i in range(3)]
        _ps6 = top.enter_context(nc.psum_tensor("ps6", [128, 512], F32))
        PS = [PS2[i // 2][:, (i % 2) * 512:(i % 2 + 1) * 512] for i in range(6)] + [_ps6[:, :]]
        BPS = [Buf(f"ps{i}") for i in range(7)]
        PSB = top.enter_context(nc.psum_tensor("psb", [128, 1024], BF16))
        _bpsb = Buf("psb")
        BPSB = [_bpsb, _bpsb]

        p.dma("sp", ident[:], ident_d[:, :], [], [Bident], Bident)
        p.dma("sp", masks[:], masks_d[:, :, :], [], [Bmasks], Bmasks)
        p.op("dve", lambda e: e.tensor_copy(identb[:], ident[:]), [Bident], [Bidentb])
        p.op("dve", lambda e: e.memset(ones_b[:], 1.0), [], [Bones])
        p.op("dve", lambda e: e.memset(ones_f[:], 1.0), [], [Bonesf])
        with ExitStack() as st:
            ra = sb(st, "ra", [128, 128]); rb = sb(st, "rb", [128, 128]); Bra = Buf("ra")
            p.dma("sp", ra[:], prm_a[:, :], [], [Bra], Bra)
            p.dma("sp", rb[:], prm_b[:, :], [], [Bra], Bra)
            p.op("pe", lambda e: e.transpose(PS[0][:, 0:128], ra[:], ident[:]), [Bra, Bident], [BPS[0]])
            p.op("pe", lambda e: e.transpose(PS[1][:, 0:128], rb[:], ident[:]), [Bra, Bident], [BPS[1]])
            p.op("dve", lambda e: e.tensor_copy(PA[:], PS[0][:, 0:128]), [BPS[0]], [Bprm])
            p.op("dve", lambda e: e.tensor_copy(PB[:], PS[1][:, 0:128]), [BPS[1]], [Bprm])
            dl = sb(st, "dl", [128, 16]); Bdl = Buf("dl")
            p.op("dve", lambda e: e.tensor_tensor(dl[:], PA[:, 44:60], PA[:, 60:76], ALU.subtract), [Bprm], [Bdl])
            p.op("act", lambda e: e.activation(dl[:], dl[:], AF.Exp), [Bdl], [Bdl])
            p.op("dve", lambda e: e.tensor_scalar(dl[:], dl[:], 1.0, None, ALU.add), [Bdl], [Bdl])
            p.op("dve", lambda e: e.memset(LB[:], 0.0), [], [BLB])
            p.op("dve", lambda e: e.memset(LB[:, 0, :, :, 1:2], 1.0), [], [BLB])
            dlv = dl[:].rearrange("p (d h o) -> p d h o", d=2, o=1)
            p.op("dve", lambda e: e.reciprocal(LB[:, 1, :, :, 0:1], dlv), [Bdl, BLB], [BLB])
            p.op("dve", lambda e: e.tensor_scalar(LB[:, 1, :, :, 1:2], LB[:, 1, :, :, 0:1], -1.0, 1.0, ALU.mult, ALU.add), [BLB], [BLB])
            p.barrier()

        def gcol_mix(l, k): return PA[:, l * 16 + k: l * 16 + k + 1]
        def gcol_q(l, k): return PA[:, 32 + l * 4 + k: 32 + l * 4 + k + 1]
        def gcol_kv(l, k): return PA[:, 40 + l * 2 + k: 40 + l * 2 + k + 1]
        def gcol_outn(l): return PA[:, 76 + l: 77 + l]
        def gcol_xn(l, k): return PB[:, l * 16 + k: l * 16 + k + 1]
        def gcol_memn(l, k): return PB[:, 32 + l * 16 + k: 32 + l * 16 + k + 1]
        def gcol_ffn(l, k): return PB[:, 64 + l * 16 + k: 64 + l * 16 + k + 1]

        psrot = [0]

        def next_ps(n=6):
            i = psrot[0] % n
            psrot[0] += 1
            return PS[i], BPS[i]

        def mm_group(ps_ap, ps_buf, pairs, reads):
            def fn(e):
                n = len(pairs)
                ins = None
                for i, (l, r) in enumerate(pairs):
                    ins = e.matmul(ps_ap, l, r, start=(i == 0), stop=(i == n - 1))
                return ins
            p.ninstr += len(pairs) - 1
            p.op("pe", fn, reads, [ps_buf])

        def rstd_from_ps(ps_ap, ps_buf, out_ap, out_buf, n):
            p.op("act", lambda e: e.activation(out_ap, ps_ap, AF.Ln, bias=EPS, scale=1.0 / n), [ps_buf], [out_buf])
            p.op("act", lambda e: e.activation(out_ap, out_ap, AF.Exp, scale=-0.5), [out_buf], [out_buf])

        class WStream:
            def __init__(self, stack, name, kc, nbuf=2, width=512):
                self.kc = kc
                self.t = [sb(stack, f"{name}{i}", [128, kc, width], BF16) for i in range(nbuf)]
                self.b = [Buf(f"{name}{i}") for i in range(nbuf)]
                self.i = 0

            def load(self, w2d, c0, c1, r0=0):
                i = self.i % len(self.t)
                self.i += 1
                src = w2d[r0:r0 + self.kc * 128, c0:c1].rearrange("(k q) n -> q k n", q=128)
                p.dma("pool", self.t[i][:, :, 0:c1 - c0], src, [], [self.b[i]], self.b[i])
                return self.t[i], self.b[i]

        def norm_group(st, xt_tiles, hT, BhT, t0, gcolf, sq, Bsq, rs, Brs, BxT=None):
            xs, Bxs = xt_tiles
            for s in range(TG // SB):
                c0 = t0 + s * SB
                p.dma("sp", xs[:], xT[:, c0:c0 + SB].rearrange("(k q) t -> q k t", q=128), ([BxT[k][s] for k in range(KC)] if BxT else []), [Bxs], Bxs)
                ps, bps = PS[6], BPS[6]
                for k4 in range(4):
                    p.op("act", lambda e, k4=k4: e.activation(sq[:, k4 * 4:(k4 + 1) * 4, :], xs[:, k4 * 4:(k4 + 1) * 4, :], AF.Square), [Bxs], [Bsq[k4]])
                mm_group(ps[:, :], bps, [(ones_b[:], sq[:, k, :]) for k in range(KC)], [Bones] + Bsq)
                rstd_from_ps(ps[:, :], bps, rs[:], Brs, D)
                for k in range(KC):
                    p.op("dve", lambda e, k=k: e.scalar_tensor_tensor(hT[:, k, s * SB:(s + 1) * SB], xs[:, k, :], gcolf(k), rs[:], ALU.mult, ALU.mult),
                         [Bxs, Brs, Bprm], [BhT[s]])

        def linear_T(ws, w2d, hT, BhT, kc, blocks, evac, r0=0):
            for (c0, c1, chunks) in blocks:
                wt, bw = ws.load(w2d, c0, c1, r0)
                for (cc, m, tag) in chunks:
                    ps0, bps0 = next_ps()
                    ps1, bps1 = next_ps()
                    def fn(e, wt=wt, cc=cc, m=m, ps0=ps0, ps1=ps1):
                        ins = None
                        for k in range(kc):
                            w_ = wt[:, k, cc - c0:cc - c0 + m]
                            e.matmul(ps0[0:m, :], w_, hT[:, k, 0:SB], start=(k == 0), stop=(k == kc - 1))
                            ins = e.matmul(ps1[0:m, :], w_, hT[:, k, SB:2 * SB], start=(k == 0), stop=(k == kc - 1))
                        return ins
                    p.ninstr += 2 * kc - 1
                    p.op("pe", fn, [bw, BhT[0], BhT[1]], [bps0, bps1])
                    evac(cc, m, tag, 0, ps0, bps0)
                    evac(cc, m, tag, 1, ps1, bps1)

        class _Stop(Exception):
            pass

        def chk(tag):
            if DEBUG_STOP == tag:
                raise _Stop()

        def run_sequence(key, S):
            x_d = x_in[key]
            y_d = y_out[key]
            mem_d = mem_in[key]
            NG = S // TG

            with ExitStack() as st:
                p.reset_pool()
                xin = [sb(st, f"xin{i}", [128, D]) for i in range(4)]
                Bxin = [Buf(f"xin{i}") for i in range(4)]
                xo = [sb(st, f"xo{i}", [128, KC, SB]) for i in range(2)]
                Bxo = [Buf(f"xo{i}") for i in range(2)]
                for g in range(S // SB):
                    for j in range(4):
                        r = g * SB + j * 128
                        p.dma("sp", xin[j][:], x_d[r:r + 128, :], [], [Bxin[j]], Bxin[j])
                    o = xo[g % 2]; bo = Bxo[g % 2]
                    for k in range(KC):
                        ps, bps = next_ps()
                        def fn(e, k=k, ps=ps):
                            ins = None
                            for j in range(4):
                                ins = e.transpose(ps[:, j * 128:(j + 1) * 128], xin[j][:, k * 128:(k + 1) * 128], ident[:])
                            return ins
                        p.op("pe", fn, Bxin + [Bident], [bps])
                        eng = "dve" if k % 2 == 0 else "act"
                        if eng == "dve":
                            p.op("dve", lambda e, k=k, ps=ps, o=o: e.tensor_copy(o[:, k, :], ps[:, :]), [bps], [bo])
                        else:
                            p.op("act", lambda e, k=k, ps=ps, o=o: e.activation(o[:, k, :], ps[:, :], AF.Copy), [bps], [bo])
                    p.dma("sp", xT[:, g * SB:(g + 1) * SB].rearrange("(k q) t -> q k t", q=128), o[:], [bo], [], bo)
                p.barrier()
            chk("0")

            for l in range(2):
                with ExitStack() as st2:
                    phase_A(st2, l, S)
                    p.barrier()
                chk("A")
                with ExitStack() as st2:
                    phase_C(st2, l, S)
                    p.barrier()
                chk("C")
                with ExitStack() as st2:
                    phase_D(st2, l, S)
                    p.barrier()
                chk("D")
                with ExitStack() as st2:
                    phase_EF(st2, l, S, mem_d)
                    p.barrier()
                chk("EF")
                with ExitStack() as st2:
                    phase_G(st2, l, S)
                    p.barrier()

            with ExitStack() as st:
                p.reset_pool()
                xs = [sb(st, f"fxs{i}", [128, KC, SB]) for i in range(2)]
                Bxs = [Buf(f"fxs{i}") for i in range(2)]
                xo = [sb(st, f"fxo{i}", [128, D]) for i in range(2)]
                Bxo = [Buf(f"fxo{i}") for i in range(2)]
                yo = [sb(st, f"fyo{i}", [128, D]) for i in range(2)]
                Byo = [Buf(f"fyo{i}") for i in range(2)]
                fg = sb(st, "fg", [128, D]); Bfg = Buf("fg")
                junk = sb(st, "fjunk", [128, D]); Bjunk = Buf("fjunk")
                ss = sb(st, "fss", [128, 8]); Bss = [Buf(f"fss{i}") for i in range(8)]
                p.dma("sp", fg[:], fin_g.partition_broadcast(128), [], [Bfg], Bfg)
                it = 0
                for g in range(S // SB):
                    x = xs[g % 2]; bx = Bxs[g % 2]
                    p.dma("sp", x[:], xT[:, g * SB:(g + 1) * SB].rearrange("(k q) t -> q k t", q=128), [], [bx], bx)
                    for j in range(4):
                        o = xo[it % 2]; bo = Bxo[it % 2]; yy = yo[it % 2]; by = Byo[it % 2]
                        for k4 in range(4):
                            ps, bps = next_ps()
                            def fn(e, k4=k4, ps=ps, x=x, j=j):
                                ins = None
                                for kk in range(4):
                                    k = k4 * 4 + kk
                                    ins = e.transpose(ps[:, kk * 128:(kk + 1) * 128], x[:, k, j * 128:(j + 1) * 128], ident[:])
                                return ins
                            p.op("pe", fn, [bx, Bident], [bps])
                            p.op("dve", lambda e, k4=k4, ps=ps, o=o: e.tensor_copy(o[:, k4 * 512:(k4 + 1) * 512], ps[:, :]), [bps], [bo])
                        sc = ss[:, it % 8: it % 8 + 1]; bsc = Bss[it % 8]
                        p.op("act", lambda e, o=o, sc=sc: e.activation(junk[:], o[:], AF.Square, accum_out=sc), [bo], [Bjunk, bsc])
                        rstd_from_ps(sc, bsc, sc, bsc, D)
                        p.op("dve", lambda e, o=o, yy=yy, sc=sc: e.scalar_tensor_tensor(yy[:], o[:], sc, fg[:], ALU.mult, ALU.mult), [bo, bsc, Bfg], [by])
                        r = g * SB + j * 128
                        p.dma("sp", y_d[r:r + 128, :], yy[:], [by], [], by)
                        it += 1
                p.barrier()

        def phase_A(st, l, S):
            p.reset_pool()
            NG = S // TG
            xs = sb(st, "a_xs", [128, KC, SB]); Bxs = Buf("a_xs")
            sq = sb(st, "a_sq", [128, KC, SB], BF16); Bsq = [Buf(f"a_sq{i}") for i in range(4)]
            rs = sb(st, "a_rs", [128, SB]); Brs = Buf("a_rs")
            hT = sb(st, "a_hT", [128, KC, TG], BF16); BhT = [Buf("a_hT0"), Buf("a_hT1")]
            ws = WStream(st, "a_w", KC)
            wq = sb(st, "a_wq", [128, 4, 1536], BF16); Bwq = Buf("a_wq")
            cq = sb(st, "a_cq", [128, 4, TG]); Bcq = [Buf("a_cq0"), Buf("a_cq1")]
            cqn = sb(st, "a_cqn", [128, 4, TG], BF16); Bcqn = [Buf("a_cqn0"), Buf("a_cqn1")]
            ckv = sb(st, "a_ckv", [128, 2, TG]); Bckv = [Buf("a_ckv0"), Buf("a_ckv1")]
            cs = sb(st, "a_cs", [64, 2, TG]); Bcs = Buf("a_cs")
            NST = 6
            stg = [sb(st, f"a_stg{i}", [128, SB]) for i in range(NST)]; Bstg = [Buf(f"a_stg{i}") for i in range(NST)]
            stgb = [sb(st, f"a_stgb{i}", [128, SB], BF16) for i in range(NST)]; Bstgb = [Buf(f"a_stgb{i}") for i in range(NST)]
            tmp = [sb(st, f"a_tmp{i}", [128, SB]) for i in range(4)]; Btmp = [Buf(f"a_tmp{i}") for i in range(4)]
            rot = {"s": 0, "b": 0, "t": 0}

            def nstg():
                i = rot["s"] % NST; rot["s"] += 1
                return stg[i], Bstg[i]

            def nstgb():
                i = rot["b"] % NST; rot["b"] += 1
                return stgb[i], Bstgb[i]

            def ntmp():
                i = rot["t"] % 4; rot["t"] += 1
                return tmp[i], Btmp[i]

            p.dma("pool", wq[:], w_uq[l].rearrange("(k q) n -> q k n", q=128), [], [Bwq], Bwq)

            for g in range(NG):
                t0 = g * TG
                norm_group(st, (xs, Bxs), hT, BhT, t0, lambda k: gcol_mix(l, k), sq, Bsq, rs, Brs)
                p.dma("sp", cs[:, 0, :], cos_t[:, t0:t0 + TG], [], [Bcs], Bcs)
                p.dma("sp", cs[:, 1, :], sin_t[:, t0:t0 + TG], [], [Bcs], Bcs)
                kr_hold = {}

                def evac(cc, m, tag, s, ps, bps):
                    tsl = slice(t0 + s * SB, t0 + (s + 1) * SB)
                    if tag == "cq":
                        k = (cc - OFF_CQ) // 128
                        p.op("act", lambda e: e.activation(cq[:, k, s * SB:(s + 1) * SB], ps[:, :], AF.Copy), [bps], [Bcq[s]])
                    elif tag == "ckv":
                        k = (cc - OFF_CKV) // 128
                        p.op("act", lambda e: e.activation(ckv[:, k, s * SB:(s + 1) * SB], ps[:, :], AF.Copy), [bps], [Bckv[s]])
                    elif tag == "kr":
                        t, bt = ntmp()
                        p.op("dve", lambda e: e.tensor_tensor(t[0:64, :], ps[0:64, :], cs[:, 0, s * SB:(s + 1) * SB], ALU.mult), [bps, Bcs], [bt])
                        kr_hold[s] = (t, bt)
                    elif tag == "krrot":
                        t, bt = kr_hold[s]
                        t2, bt2 = ntmp()
                        p.op("dve", lambda e: e.tensor_tensor(t2[0:64, :], ps[0:64, :], cs[:, 1, s * SB:(s + 1) * SB], ALU.mult), [bps, Bcs], [bt2])
                        o, bo = nstgb()
                        p.op("dve", lambda e: e.tensor_tensor(o[0:64, :], t[0:64, :], t2[0:64, :], ALU.add), [bt, bt2], [bo])
                        p.dma("sp", latT[256:320, tsl], o[0:64, :], [bo], [], bo)
                    elif tag == "hq" or tag == "hg":
                        h = (cc - (OFF_HQ if tag == "hq" else OFF_HG)) // 128
                        t, bt = ntmp()
                        p.op("act", lambda e: e.activation(t[:], ps[:, :], AF.Exp, scale=-1.0), [bps], [bt])
                        p.op("act", lambda e: e.activation(t[:], t[:], AF.Ln, bias=1.0), [bt], [bt], self_ok=True)
                        p.op("act", lambda e: e.activation(t[:], t[:], AF.Exp, scale=-1.0), [bt], [bt], self_ok=True)
                        o, bo = nstg()
                        p.op("dve", lambda e: e.tensor_tensor(o[:], ps[:, :], t[:], ALU.mult), [bps, bt], [bo])
                        dst = hqT if tag == "hq" else sgT
                        p.dma("sp", dst[h * 128:(h + 1) * 128, tsl], o[:], [bo], [], bo)
                    elif tag == "hff" or tag == "hfb":
                        d = 0 if tag == "hff" else 1
                        h = (cc - (OFF_HFF if d == 0 else OFF_HFB)) // 128
                        t, bt = ntmp()
                        p.op("act", lambda e: e.activation(t[:], ps[:, :], AF.Exp, scale=-1.0), [bps], [bt])
                        p.op("act", lambda e: e.activation(t[:], t[:], AF.Ln, bias=1.0), [bt], [bt], self_ok=True)
                        p.op("act", lambda e: e.activation(t[:], t[:], AF.Exp, scale=-1.0), [bt], [bt], self_ok=True)
                        p.op("dve", lambda e: e.tensor_scalar(t[:], t[:], LB[:, l, d, h, 1:2], LB[:, l, d, h, 0:1], ALU.mult, ALU.add), [bt, BLB], [bt])
                        o, bo = nstg()
                        p.op("act", lambda e: e.activation(o[:], t[:], AF.Ln), [bt], [bo])
                        p.dma("sp", lfT[d][h * 128:(h + 1) * 128, tsl], o[:], [bo], [], bo)
                        o2, bo2 = nstg()
                        p.op("act", lambda e: e.activation(o2[:], t[:], AF.Copy, scale=-1.0, bias=1.0), [bt], [bo2])
                        p.dma("sp", kkT[d][h * 128:(h + 1) * 128, tsl], o2[:], [bo2], [], bo2)

                def ch(off, n, tag):
                    return [(off + i * 128, 128, tag) for i in range(n)]

                blocks = [(0, 512, ch(0, 4, "cq")),
                          (512, 832, ch(512, 2, "ckv") + [(768, 64, "kr")])]
                linear_T(ws, w_in[l], hT, BhT, KC, blocks, evac)
                wt, bw = ws.load(w_in[l], 768, 832)
                for s in range(TG // SB):
                    ps, bps = next_ps()
                    def fn(e, s=s, ps=ps, wt=wt):
                        ins = None
                        for half in range(2):
                            src = 32 if half == 0 else 0
                            for k in range(KC):
                                ins = e.matmul(ps[half * 32:(half + 1) * 32, :], wt[:, k, src:src + 32], hT[:, k, s * SB:(s + 1) * SB], start=(k == 0), stop=(k == KC - 1))
                        return ins
                    p.ninstr += 31
                    p.op("pe", fn, [bw, BhT[s]], [bps])
                    evac(768, 64, "krrot", s, ps, bps)
                blocks = []
                for (off, tag) in ((OFF_HQ, "hq"), (OFF_HFF, "hff"), (OFF_HFB, "hfb"), (OFF_HG, "hg")):
                    blocks.append((off, off + 512, ch(off, 4, tag)))
                    blocks.append((off + 512, off + 1024, ch(off + 512, 4, tag)))
                linear_T(ws, w_in[l], hT, BhT, KC, blocks, evac)
                for cb in range(2):
                    wt, bw = ws.load(w_in[l], OFF_HI + cb * 512, OFF_HI + (cb + 1) * 512)
                    for tt in range(TG // 128):
                        ps, bps = next_ps()
                        s = tt // 4
                        mm_group(ps[:, :], bps, [(hT[:, k, tt * 128:(tt + 1) * 128], wt[:, k, :]) for k in range(KC)], [bw, BhT[s]])
                        o, bo = nstgb()
                        p.op("act", lambda e, o=o, ps=ps: e.activation(o[:], ps[:, :], AF.Copy), [bps], [bo])
                        r = t0 + tt * 128
                        p.dma("sp", vtok[cb * 4:(cb + 1) * 4, r:r + 128, :].rearrange("h t c -> t h c"), o[:].rearrange("t (h c) -> t h c", h=4), [bo], [], bo)
                for s in range(TG // SB):
                    ssl = slice(s * SB, (s + 1) * SB)
                    for k in range(4):
                        p.op("act", lambda e, k=k: e.activation(sq[:, k, :], cq[:, k, ssl], AF.Square), [Bcq[s]], [Bsq[0]])
                    ps, bps = PS[6], BPS[6]
                    mm_group(ps[:, :], bps, [(ones_b[:], sq[:, k, :]) for k in range(4)], [Bones, Bsq[0]])
                    rstd_from_ps(ps[:, :], bps, rs[:], Brs, QL)
                    for k in range(4):
                        p.op("dve", lambda e, k=k: e.scalar_tensor_tensor(cqn[:, k, ssl], cq[:, k, ssl], gcol_q(l, k), rs[:], ALU.mult, ALU.mult), [Bcq[s], Brs, Bprm], [Bcqn[s]])
                    for k in range(2):
                        p.op("act", lambda e, k=k: e.activation(sq[:, 4 + k, :], ckv[:, k, ssl], AF.Square), [Bckv[s]], [Bsq[1]])
                    mm_group(ps[:, :], bps, [(ones_b[:], sq[:, 4 + k, :]) for k in range(2)], [Bones, Bsq[1]])
                    rstd_from_ps(ps[:, :], bps, rs[:], Brs, KVL)
                    for k in range(2):
                        o, bo = nstgb()
                        p.op("dve", lambda e, k=k, o=o: e.scalar_tensor_tensor(o[:], ckv[:, k, ssl], gcol_kv(l, k), rs[:], ALU.mult, ALU.mult), [Bckv[s], Brs, Bprm], [bo])
                        p.dma("sp", latT[k * 128:(k + 1) * 128, t0 + s * SB:t0 + (s + 1) * SB], o[:], [bo], [], bo)
                for h in range(H):
                    for s in range(TG // SB):
                        ssl = slice(s * SB, (s + 1) * SB)
                        tsl = slice(t0 + s * SB, t0 + (s + 1) * SB)
                        c0 = h * 192
                        ps, bps = next_ps()
                        mm_group(ps[:, :], bps, [(wq[:, k, c0:c0 + 128], cqn[:, k, ssl]) for k in range(4)], [Bwq, Bcqn[s]])
                        o, bo = nstgb()
                        p.op("act", lambda e, o=o, ps=ps: e.activation(o[:], ps[:, :], AF.Copy), [bps], [bo])
                        p.dma("sp", qT[c0:c0 + 128, tsl], o[:], [bo], [], bo)
                        ps, bps = next_ps()
                        mm_group(ps[0:64, :], bps, [(wq[:, k, c0 + 128:c0 + 192], cqn[:, k, ssl]) for k in range(4)], [Bwq, Bcqn[s]])
                        psr, bpsr = next_ps()
                        def fn(e, psr=psr, c0=c0, ssl=ssl):
                            ins = None
                            for (po, src) in ((0, 160), (32, 128)):
                                for k in range(4):
                                    ins = e.matmul(psr[po:po + 32, :], wq[:, k, c0 + src:c0 + src + 32], cqn[:, k, ssl], start=(k == 0), stop=(k == 3))
                            return ins
                        p.ninstr += 7
                        p.op("pe", fn, [Bwq, Bcqn[s]], [bpsr])
                        t, bt = ntmp()
                        p.op("dve", lambda e, t=t, ps=ps: e.tensor_tensor(t[0:64, :], ps[0:64, :], cs[:, 0, ssl], ALU.mult), [bps, Bcs], [bt])
                        t2, bt2 = ntmp()
                        p.op("dve", lambda e, t2=t2, psr=psr: e.tensor_tensor(t2[0:64, :], psr[0:64, :], cs[:, 1, ssl], ALU.mult), [bpsr, Bcs], [bt2])
                        o, bo = nstgb()
                        p.op("dve", lambda e, o=o, t=t, t2=t2: e.tensor_tensor(o[0:64, :], t[0:64, :], t2[0:64, :], ALU.add), [bt, bt2], [bo])
                        p.dma("sp", qT[c0 + 128:c0 + 192, tsl], o[0:64, :], [bo], [], bo)

        def phase_C(st, l, S):
            p.reset_pool()
            NR = S // SB
            NB = 2
            def arr(name, shape, dt=F32):
                return [sb(st, f"{name}{i}", shape, dt) for i in range(NB)]
            QT = arr("c_QT", [128, H, SB], BF16); KT = arr("c_KT", [128, H, SB], BF16); QH = arr("c_QH", [128, H, SB], BF16)
            KH = arr("c_KH", [128, H, 4, 128], BF16); VV = arr("c_V", [128, H, 4, 128], BF16); DEC = arr("c_dec", [128, H, 8])
            Bprep = [[Buf(f"c_prep{i}_{h}") for h in range(H)] for i in range(NB)]
            BKH = [[Buf(f"c_kh{i}_{h}") for h in range(H)] for i in range(NB)]
            BV = [[Buf(f"c_v{i}_{h}") for h in range(H)] for i in range(NB)]
            OT = arr("c_OT", [128, H, SB]); BOT = [[Buf(f"c_ot{i}_{h}") for h in range(H)] for i in range(NB)]
            Sf = sb(st, "c_Sf", [128, H, 128]); Sb_ = sb(st, "c_Sb", [128, H, 128], BF16)
            BSf = [Buf(f"c_sf{h}") for h in range(H)]; BSb = [Buf(f"c_sb{h}") for h in range(H)]
            NL = 3
            lf = [sb(st, f"c_lf{i}", [128, SB]) for i in range(NL)]; Blf = [Buf(f"c_lf{i}") for i in range(NL)]
            kk = [sb(st, f"c_kk{i}", [128, SB]) for i in range(NL)]; Bkk = [Buf(f"c_kk{i}") for i in range(NL)]
            hq = [sb(st, f"c_hq{i}", [128, SB]) for i in range(NL)]; Bhq = [Buf(f"c_hq{i}") for i in range(NL)]
            G = [sb(st, f"c_G{i}", [128, SB]) for i in range(2)]; BG = [Buf(f"c_G{i}") for i in range(2)]
            T1 = [sb(st, f"c_T1{i}", [128, SB]) for i in range(2)]; BT1 = [Buf(f"c_T1{i}") for i in range(2)]
            T2 = [sb(st, f"c_T2{i}", [128, SB]) for i in range(2)]; BT2 = [Buf(f"c_T2{i}") for i in range(2)]
            E = [sb(st, f"c_E{i}", [128, SB]) for i in range(4)]; BE = [Buf(f"c_E{i}") for i in range(4)]
            khT = [sb(st, f"c_khT{i}", [128, SB], BF16) for i in range(2)]; BkhT = [Buf(f"c_khT{i}") for i in range(2)]
            sc = [sb(st, f"c_sc{i}", [128, 64], BF16) for i in range(8)]; Bsc = [Buf(f"c_sc{i}") for i in range(8)]
            of_l = [sb(st, f"c_ofl{i}", [128, SB]) for i in range(2)]; Bofl = [Buf(f"c_ofl{i}") for i in range(2)]
            sg_l = [sb(st, f"c_sgl{i}", [128, SB]) for i in range(2)]; Bsgl = [Buf(f"c_sgl{i}") for i in range(2)]
            osq = [sb(st, f"c_osq{i}", [128, SB], BF16) for i in range(2)]; Bosq = [Buf(f"c_osq{i}") for i in range(2)]
            ors = [sb(st, f"c_ors{i}", [128, SB]) for i in range(2)]; Bors = [Buf(f"c_ors{i}") for i in range(2)]
            omx = [sb(st, f"c_omx{i}", [128, SB], BF16) for i in range(2)]; Bomx = [Buf(f"c_omx{i}") for i in range(2)]
            Bsc_ps = [Buf(f"c_scps{i}") for i in range(8)]
            Bo_ps = [Buf(f"c_ops{i}") for i in range(8)]
            Bu_ps = [Buf(f"c_ups{i}") for i in range(8)]
            cnt = {"lf": 0, "g": 0, "e": 0, "kh": 0, "sc": 0, "scps": 0, "ops": 0, "ups": 0, "fin": 0}

            def prep(d, r, bi, h):
                tsl = slice(r * SB, (r + 1) * SB)
                rows = slice(h * 128, (h + 1) * 128)
                i = cnt["lf"] % NL; cnt["lf"] += 1
                p.dma("sp", lf[i][:], lfT[d][rows, tsl], [], [Blf[i]], Blf[i])
                p.dma("sp", kk[i][:], kkT[d][rows, tsl], [], [Bkk[i]], Bkk[i])
                p.dma("sp", hq[i][:], hqT[rows, tsl], [], [Bhq[i]], Bhq[i])
                p.dma("sp", VV[bi][:, h, :, :], vtok[h, r * SB:(r + 1) * SB, :].rearrange("(j q) c -> q j c", q=128), [], [BV[bi][h]], BV[bi][h])
                gi = cnt["g"] % 2; cnt["g"] += 1
                g_, bg = G[gi], BG[gi]; t1, bt1 = T1[gi], BT1[gi]; t2, bt2 = T2[gi], BT2[gi]
                g3 = g_[:].rearrange("q (c t) -> q c t", t=64)
                t13 = t1[:].rearrange("q (c t) -> q c t", t=64)
                t23 = t2[:].rearrange("q (c t) -> q c t", t=64)
                p.op("dve", lambda e: e.tensor_tensor_scan(g_[:], masks[:, 2, :], lf[i][:], 0.0, ALU.mult, ALU.add), [Bmasks, Blf[i]], [bg])
                if d == 1:
                    p.op("pool", lambda e: e.tensor_tensor(t1[:], lf[i][:], g_[:], ALU.subtract), [Blf[i], bg], [bt1])
                    p.op("dve", lambda e: e.tensor_tensor(g3, t13, g3[:, :, 63:64].to_broadcast([128, 8, 64]), ALU.add), [bt1, bg], [bg])
                    mid = 32; far = 0
                else:
                    mid = 31; far = 63
                p.op("dve", lambda e: e.tensor_tensor(t13, g3, g3[:, :, mid:mid + 1].to_broadcast([128, 8, 64]), ALU.subtract), [bg], [bt1])
                p.op("pool", lambda e: e.tensor_tensor(t23, g3[:, :, far:far + 1].to_broadcast([128, 8, 64]), g3, ALU.subtract), [bg], [bt2])
                es = []
                for _ in range(4):
                    ei = cnt["e"] % 4; cnt["e"] += 1
                    es.append((E[ei], BE[ei]))
                (e1, be1), (e2, be2), (e3, be3), (e4, be4) = es
                p.op("act", lambda e: e.activation(e1[:], t1[:], AF.Exp, bias=LN_QS), [bt1], [be1])
                p.op("act", lambda e: e.activation(e2[:], t1[:], AF.Exp, scale=-1.0), [bt1], [be2])
                p.op("act", lambda e: e.activation(e3[:], g_[:], AF.Exp, bias=LN_QS), [bg], [be3])
                p.op("act", lambda e: e.activation(e4[:], t2[:], AF.Exp), [bt2], [be4])
                p.op("act", lambda e: e.activation(DEC[bi][:, h, :], g3[:, :, far], AF.Exp), [bg], [Bprep[bi][h]])
                p.op("dve", lambda e: e.tensor_tensor(QT[bi][:, h, :], hq[i][:], e1[:], ALU.mult), [Bhq[i], be1], [Bprep[bi][h]])
                p.op("pool", lambda e: e.tensor_tensor(KT[bi][:, h, :], kk[i][:], e2[:], ALU.mult), [Bkk[i], be2], [Bprep[bi][h]])
                p.op("dve", lambda e: e.tensor_tensor(QH[bi][:, h, :], hq[i][:], e3[:], ALU.mult), [Bhq[i], be3], [Bprep[bi][h]])
                ki = cnt["kh"] % 2; cnt["kh"] += 1
                p.op("pool", lambda e: e.tensor_tensor(khT[ki][:], kk[i][:], e4[:], ALU.mult), [Bkk[i], be4], [BkhT[ki]])
                def fn(e):
                    ins = None
                    for j in range(4):
                        ins = e.transpose(PSB[:, ki * 512 + j * 128: ki * 512 + (j + 1) * 128], khT[ki][:, j * 128:(j + 1) * 128], identb[:])
                    return ins
                p.op("pe", fn, [BkhT[ki], Bidentb], [BPSB[ki]])
                p.op("act", lambda e: e.activation(KH[bi][:, h, :, :], PSB[:, ki * 512:(ki + 1) * 512].rearrange("q (j c) -> q j c", j=4), AF.Copy), [BPSB[ki]], [BKH[bi][h]])

            def finish(d, r, bi, h):
                tsl = slice(r * SB, (r + 1) * SB)
                rows = slice(h * 128, (h + 1) * 128)
                if d == 0:
                    p.dma("sp", ofT[rows, tsl], OT[bi][:, h, :], [BOT[bi][h]], [], BOT[bi][h])
                    return
                fi = cnt["fin"] % 2; cnt["fin"] += 1
                p.dma("sp", of_l[fi][:], ofT[rows, tsl], [], [Bofl[fi]], Bofl[fi])
                p.dma("sp", sg_l[fi][:], sgT[rows, tsl], [], [Bsgl[fi]], Bsgl[fi])
                o = OT[bi][:, h, :]
                p.op("pool", lambda e: e.tensor_tensor(o, o, of_l[fi][:], ALU.add), [BOT[bi][h], Bofl[fi]], [BOT[bi][h]])
                p.op("act", lambda e: e.activation(osq[fi][:], o, AF.Square), [BOT[bi][h]], [Bosq[fi]])
                ps, bps = PS[6], BPS[6]
                mm_group(ps[:, :], bps, [(ones_b[:], osq[fi][:])], [Bones, Bosq[fi]])
                rstd_from_ps(ps[:, :], bps, ors[fi][:], Bors[fi], 128)
                p.op("dve", lambda e: e.scalar_tensor_tensor(ors[fi][:], ors[fi][:], gcol_outn(l), sg_l[fi][:], ALU.mult, ALU.mult), [Bors[fi], Bsgl[fi], Bprm], [Bors[fi]])
                p.op("dve", lambda e: e.tensor_tensor(omx[fi][:], o, ors[fi][:], ALU.mult), [BOT[bi][h], Bors[fi]], [Bomx[fi]])
                p.dma("sp", mixT[1024 + h * 128: 1024 + (h + 1) * 128, tsl], omx[fi][:], [Bomx[fi]], [], Bomx[fi])

            scA = [sb(st, f"c_scA{i}", [128, H, 64], BF16) for i in range(2)]; BscA = [Buf(f"c_scA{i}") for i in range(2)]
            BSfh = [Buf("c_sfh0"), Buf("c_sfh1")]; BSbh = [Buf("c_sbh0"), Buf("c_sbh1")]
            step = 0
            for d in range(2):
                order = list(range(NR)) if d == 0 else list(range(NR - 1, -1, -1))
                corder = list(range(8)) if d == 0 else list(range(7, -1, -1))
                mrow = d
                p.op("dve", lambda e: e.memset(Sf[:], 0.0), [], BSf)
                p.op("pool", lambda e: e.memset(Sb_[:], 0.0), [], BSbh)
                for h in range(H):
                    prep(d, order[0], 0, h)
                for ri, r in enumerate(order):
                    bi = ri % 2
                    for ci, c in enumerate(corder):
                        j = c // 2
                        pb = (c % 2) * 64
                        csl = slice(c * 64, (c + 1) * 64)
                        x = step % 2
                        step += 1
                        psc, bpsc = PS[x], BPS[x]
                        pso, bpso = PS[2 + x], BPS[2 + x]
                        def fsc(e, psc=psc, bi=bi, csl=csl, pb=pb):
                            ins = None
                            for h in range(H):
                                ins = e.matmul(psc[pb:pb + 64, h * 64:(h + 1) * 64], KT[bi][:, h, csl], QT[bi][:, h, csl], start=True, stop=True)
                            return ins
                        p.ninstr += 7
                        p.op("pe", fsc, Bprep[bi], [bpsc])
                        sca, bsca = scA[x], BscA[x]
                        p.op("dve", lambda e, psc=psc, sca=sca, pb=pb: e.tensor_tensor(sca[pb:pb + 64, :, :], psc[pb:pb + 64, :].rearrange("q (h t) -> q h t", h=H),
                                                                                 masks[pb:pb + 64, mrow:mrow + 1, 0:64].to_broadcast([64, H, 64]), ALU.mult),
                             [bpsc, Bmasks], [bsca])
                        def fo(e, pso=pso, sca=sca, bi=bi, csl=csl, pb=pb, j=j):
                            ins = None
                            for h in range(H):
                                e.matmul(pso[:, h * 64:(h + 1) * 64], VV[bi][pb:pb + 64, h, j, :], sca[pb:pb + 64, h, :], start=True, stop=False)
                                ins = e.matmul(pso[:, h * 64:(h + 1) * 64], Sb_[:, h, :], QH[bi][:, h, csl], start=False, stop=True)
                            return ins
                        p.ninstr += 15
                        p.op("pe", fo, BV[bi] + [bsca] + BSbh + Bprep[bi], [bpso])
                        for hb in range(2):
                            def fu(e, hb=hb, bi=bi, pb=pb, j=j):
                                ins = None
                                for hh in range(4):
                                    h = hb * 4 + hh
                                    ins = e.matmul(PS[4 + hb][:, hh * 128:(hh + 1) * 128], KH[bi][pb:pb + 64, h, j, :], VV[bi][pb:pb + 64, h, j, :], start=True, stop=True)
                                return ins
                            p.ninstr += 3
                            p.op("pe", fu, BKH[bi][hb * 4:hb * 4 + 4] + BV[bi][hb * 4:hb * 4 + 4], [BPS[4 + hb]])
                        p.op("act", lambda e, pso=pso, bi=bi, csl=csl: e.activation(OT[bi][:, :, csl], pso[:, :].rearrange("q (h t) -> q h t", h=H), AF.Copy), [bpso], BOT[bi])
                        for hb in range(2):
                            for hh in range(4):
                                h = hb * 4 + hh
                                p.op("dve", lambda e, h=h, hb=hb, hh=hh, bi=bi, c=c: e.scalar_tensor_tensor(Sf[:, h, :], Sf[:, h, :], DEC[bi][:, h, c:c + 1], PS[4 + hb][:, hh * 128:(hh + 1) * 128], ALU.mult, ALU.add),
                                     [BSf[h], Bprep[bi][h], BPS[4 + hb]], [BSf[h]])
                            p.op("act", lambda e, hb=hb: e.activation(Sb_[:, hb * 4:hb * 4 + 4, :], Sf[:, hb * 4:hb * 4 + 4, :], AF.Copy), BSf[hb * 4:hb * 4 + 4], [BSbh[hb]])
                        if ri + 1 < NR:
                            prep(d, order[ri + 1], (ri + 1) % 2, ci)
                    for h in range(H):
                        finish(d, r, bi, h)
                p.barrier()

        def phase_D(st, l, S):
            p.reset_pool()
            NKB = S // 128
            NQB = S // SB
            KhT = sb(st, "d_KhT", [128, S], BF16); BKhT = Buf("d_KhT")
            Vh = sb(st, "d_Vh", [128, NKB, 128], BF16); BVh = Buf("d_Vh")
            krT = sb(st, "d_krT", [64, S], BF16); BkrT = Buf("d_krT")
            wkv = sb(st, "d_wkv", [128, 2, 2048], BF16); Bwkv = Buf("d_wkv")
            lat = [sb(st, f"d_lat{i}", [128, 2, SB], BF16) for i in range(3)]; Blat = [Buf(f"d_lat{i}") for i in range(3)]
            qn = [sb(st, f"d_qn{i}", [128, SB], BF16) for i in range(2)]; Bqn = [Buf(f"d_qn{i}") for i in range(2)]
            qr = [sb(st, f"d_qr{i}", [64, SB], BF16) for i in range(2)]; Bqr = [Buf(f"d_qr{i}") for i in range(2)]
            NE = 3
            Et = [sb(st, f"d_E{i}", [128, 2 * SB], BF16) for i in range(NE)]; BEt = [Buf(f"d_E{i}") for i in range(NE)]
            acc = [sb(st, f"d_acc{i}", [128, 2 * SB]) for i in range(2)]; Bacc = [Buf(f"d_acc{i}") for i in range(2)]
            rc = [sb(st, f"d_rc{i}", [128, SB]) for i in range(2)]; Brc = [Buf(f"d_rc{i}") for i in range(2)]
            ao = [sb(st, f"d_ao{i}", [128, SB], BF16) for i in range(2)]; Bao = [Buf(f"d_ao{i}") for i in range(2)]
            p.dma("pool", wkv[:], w_ukv[l].rearrange("(k q) n -> q k n", q=128), [], [Bwkv], Bwkv)
            p.dma("sp", krT[:], latT[256:320, 0:S], [], [BkrT], BkrT)
            li = 0
            qi = 0
            ei = 0
            for h in range(H):
                kc0 = h * 256
                vc0 = h * 256 + 128
                for blk in range(S // SB):
                    lt = lat[li % 3]; bl = Blat[li % 3]; li += 1
                    p.dma("sp", lt[:], latT[0:256, blk * SB:(blk + 1) * SB].rearrange("(k q) t -> q k t", q=128), [], [bl], bl)
                    ps, bps = PS[blk % 2], BPS[blk % 2]
                    mm_group(ps[:, :], bps, [(wkv[:, k, kc0:kc0 + 128], lt[:, k, :]) for k in range(2)], [Bwkv, bl])
                    p.op("dve", lambda e, ps=ps, blk=blk: e.tensor_copy(KhT[:, blk * SB:(blk + 1) * SB], ps[:, :]), [bps], [BKhT])
                    ps2, bps2 = PS[2 + blk % 2], BPS[2 + blk % 2]
                    def fn(e, ps2=ps2, lt=lt):
                        ins = None
                        for j in range(4):
                            for k in range(2):
                                ins = e.matmul(ps2[:, j * 128:(j + 1) * 128], lt[:, k, j * 128:(j + 1) * 128], wkv[:, k, vc0:vc0 + 128], start=(k == 0), stop=(k == 1))
                        return ins
                    p.ninstr += 7
                    p.op("pe", fn, [Bwkv, bl], [bps2])
                    p.op("act", lambda e, ps2=ps2, blk=blk: e.activation(Vh[:, blk * 4:(blk + 1) * 4, :], ps2[:, :].rearrange("q (j c) -> q j c", j=4), AF.Copy), [bps2], [BVh])
                for qb in range(NQB):
                    q1 = qn[qi % 2]; bq1 = Bqn[qi % 2]; q2 = qr[qi % 2]; bq2 = Bqr[qi % 2]; qi += 1
                    p.dma("sp", q1[:], qT[h * 192:h * 192 + 128, qb * SB:(qb + 1) * SB], [], [bq1], bq1)
                    p.dma("sp", q2[:], qT[h * 192 + 128:h * 192 + 192, qb * SB:(qb + 1) * SB], [], [bq2], bq2)
                    po, bpo = PS[4 + qb % 2], BPS[4 + qb % 2]
                    pz, bpz = PS[6], BPS[6]
                    ac, bac = acc[qb % 2], Bacc[qb % 2]
                    NSS = NKB // 2
                    pend = []

                    def pv(ss, et, bet):
                        def fn(e):
                            e.matmul(po, Vh[:, 2 * ss, :], et[:, 0:SB], start=(ss == 0), stop=False)
                            return e.matmul(po, Vh[:, 2 * ss + 1, :], et[:, SB:2 * SB], start=False, stop=(ss == NSS - 1))
                        p.ninstr += 1
                        p.op("pe", fn, [BVh, bet], [bpo])
                        if ss == 0:
                            p.op("dve", lambda e: e.tensor_copy(ac[:], et[:]), [bet], [bac])
                        else:
                            p.op("dve", lambda e: e.tensor_tensor(ac[:], ac[:], et[:], ALU.add), [bet, bac], [bac], self_ok=True)

                    for ss in range(NSS):
                        x = ss % 2
                        def fsc(e, ss=ss, x=x):
                            ins = None
                            for i in range(2):
                                kb = 2 * ss + i
                                e.matmul(PS[2 * x + i], KhT[:, kb * 128:(kb + 1) * 128], q1[:], start=True, stop=False)
                                ins = e.matmul(PS[2 * x + i], krT[:, kb * 128:(kb + 1) * 128], q2[:], start=False, stop=True)
                            return ins
                        p.ninstr += 3
                        p.op("pe", fsc, [BKhT, BkrT, bq1, bq2], [BPS[2 * x], BPS[2 * x + 1]])
                        et = Et[ei % NE]; bet = BEt[ei % NE]; ei += 1
                        p.op("act", lambda e, et=et, x=x: e.activation(et[:], PS2[x][:, :], AF.Exp, scale=ATT_SCALE), [BPS[2 * x], BPS[2 * x + 1]], [bet])
                        pend.append((ss, et, bet))
                        if len(pend) > 1:
                            pv(*pend.pop(0))
                    while pend:
                        pv(*pend.pop(0))
                    mm_group(pz, bpz, [(ones_f[:], ac[:, 0:SB]), (ones_f[:], ac[:, SB:2 * SB])], [Bonesf, bac])
                    r_, br = rc[qb % 2], Brc[qb % 2]
                    a_, ba = ao[qb % 2], Bao[qb % 2]
                    p.op("dve", lambda e, r_=r_, pz=pz: e.reciprocal(r_[:], pz), [bpz], [br])
                    p.op("dve", lambda e, a_=a_, r_=r_, po=po: e.tensor_tensor(a_[:], po, r_[:], ALU.mult), [bpo, br], [ba])
                    p.dma("sp", mixT[h * 128:(h + 1) * 128, qb * SB:(qb + 1) * SB], a_[:], [ba], [], ba)

        def phase_EF(st, l, S, mem_d):
            p.reset_pool()
            NG = S // TG
            kmT = sb(st, "kmT", [128, KC, NMEM], BF16); BkmT = Buf("kmT")
            vm = sb(st, "vm", [128, 2, D], BF16); Bvm = Buf("vm")
            with ExitStack() as st2:
                mt = sb(st2, "mt", [128, 2, D]); Bmt = Buf("mt")
                mb = sb(st2, "mb", [128, 2, D], BF16); Bmb = Buf("mb")
                mT = sb(st2, "mT", [128, KC, NMEM], BF16); BmT = Buf("mT")
                junk = sb(st2, "junk", [128, D]); Bjunk = Buf("junk")
                ssm = sb(st2, "ssm", [128, 2]); Bssm = Buf("ssm")
                wsm = WStream(st2, "wsm", KC)
                p.dma("sp", mt[:], mem_d.rearrange("(j q) c -> q j c", q=128), [], [Bmt], Bmt)
                for j in range(2):
                    p.op("act", lambda e, j=j: e.activation(junk[:], mt[:, j, :], AF.Square, accum_out=ssm[:, j:j + 1]), [Bmt], [Bjunk, Bssm])
                rstd_from_ps(ssm[:], Bssm, ssm[:], Bssm, D)
                for j in range(2):
                    p.op("dve", lambda e, j=j: e.tensor_scalar(mb[:, j, :], mt[:, j, :], ssm[:, j:j + 1], None, ALU.mult), [Bmt, Bssm], [Bmb])
                for k in range(KC):
                    hb = k % 2
                    def fn(e, k=k, hb=hb):
                        ins = None
                        for j in range(2):
                            ins = e.transpose(PSB[:, hb * 512 + j * 128: hb * 512 + (j + 1) * 128], mb[:, j, k * 128:(k + 1) * 128], identb[:])
                        return ins
                    p.op("pe", fn, [Bmb, Bidentb], [BPSB[hb]])
                    p.op("act", lambda e, k=k, hb=hb: e.activation(mT[:, k, :], PSB[:, hb * 512: hb * 512 + 256], AF.Copy, scale=gcol_memn(l, k)),
                         [BPSB[hb], Bprm], [BmT])
                for cb in range(4):
                    wt, bw = wsm.load(w_xk[l], cb * 512, (cb + 1) * 512)
                    for c4 in range(4):
                        ps, bps = next_ps()
                        mm_group(ps[:, 0:NMEM], bps, [(wt[:, k, c4 * 128:(c4 + 1) * 128], mT[:, k, :]) for k in range(KC)], [bw, BmT])
                        p.op("dve", lambda e, ps=ps, c=cb * 4 + c4: e.tensor_copy(kmT[:, c, :], ps[:, 0:NMEM]), [bps], [BkmT])
                for cb in range(4):
                    wt, bw = wsm.load(w_xv[l], cb * 512, (cb + 1) * 512)
                    for j in range(2):
                        ps, bps = next_ps()
                        mm_group(ps[:, :], bps, [(mT[:, k, j * 128:(j + 1) * 128], wt[:, k, :]) for k in range(KC)], [bw, BmT])
                        p.op("dve", lambda e, ps=ps, j=j, cb=cb: e.tensor_copy(vm[:, j, cb * 512:(cb + 1) * 512], ps[:, :]), [bps], [Bvm])
                p.barrier()
            mx = sb(st, "e_mx", [128, KC, TG], BF16); Bmx = [Buf("e_mx0"), Buf("e_mx1")]
            xs = sb(st, "e_xs", [128, KC, SB]); Bxs = Buf("e_xs")
            sq = sb(st, "e_sq", [128, KC, SB], BF16); Bsq = [Buf(f"e_sq{i}") for i in range(4)]
            rs = sb(st, "e_rs", [128, SB]); Brs = Buf("e_rs")
            hT = sb(st, "e_hT", [128, KC, TG], BF16); BhT = [Buf("e_hT0"), Buf("e_hT1")]
            qx = mx; Bqx = Bmx
            ws = WStream(st, "e_w", KC)
            xr = [sb(st, f"e_xr{i}", [128, SB]) for i in range(4)]; Bxr = [Buf(f"e_xr{i}") for i in range(4)]
            Et = [sb(st, f"e_E{i}", [128, SB], BF16) for i in range(4)]; BEt = [Buf(f"e_E{i}") for i in range(4)]
            rc = [sb(st, f"e_rc{i}", [128, SB]) for i in range(2)]; Brc = [Buf(f"e_rc{i}") for i in range(2)]
            rot = {"x": 0, "e": 0}
            blocksD = [(cb * 512, (cb + 1) * 512, [(cb * 512 + i * 128, 128, "x") for i in range(4)]) for cb in range(4)]

            def resid_evac(t0, BxT):
                def evac(cc, m, tag, s, ps, bps):
                    i = rot["x"] % 4; rot["x"] += 1
                    tsl = slice(t0 + s * SB, t0 + (s + 1) * SB)
                    bx = BxT[cc // 128][s]
                    p.dma("sp", xr[i][:], xT[cc:cc + 128, tsl], [bx], [Bxr[i]], Bxr[i])
                    p.op("dve", lambda e: e.tensor_tensor(xr[i][:], xr[i][:], ps[:, :], ALU.add), [Bxr[i], bps], [Bxr[i]])
                    p.dma("sp", xT[cc:cc + 128, tsl], xr[i][:], [Bxr[i]], [bx], Bxr[i])
                return evac

            for g in range(NG):
                t0 = g * TG
                for s in range(2):
                    p.dma("sp", mx[:, :, s * SB:(s + 1) * SB], mixT[:, t0 + s * SB:t0 + (s + 1) * SB].rearrange("(k q) t -> q k t", q=128), [], [Bmx[s]], Bmx[s])
                BxT = [[Buf(f"xT{k}_{s_}") for s_ in range(2)] for k in range(KC)]
                linear_T(ws, w_out[l], mx, Bmx, KC, blocksD, resid_evac(t0, BxT))
                norm_group(st, (xs, Bxs), hT, BhT, t0, lambda k: gcol_xn(l, k), sq, Bsq, rs, Brs, BxT)
                def evq(cc, m, tag, s, ps, bps):
                    k = cc // 128
                    p.op("act", lambda e: e.activation(qx[:, k, s * SB:(s + 1) * SB], ps[:, :], AF.Copy), [bps], [Bqx[s]])
                linear_T(ws, w_xq[l], hT, BhT, KC, blocksD, evq)
                for s in range(2):
                    ssl = slice(s * SB, (s + 1) * SB)
                    for j in range(4):
                        ets = []
                        for mbk in range(2):
                            ps, bps = next_ps(3)
                            mm_group(ps[:, :], bps, [(kmT[:, j * 4 + c, mbk * 128:(mbk + 1) * 128], qx[:, j * 4 + c, ssl]) for c in range(4)], [BkmT, Bqx[s]])
                            i = rot["e"] % 4; rot["e"] += 1
                            p.op("act", lambda e, i=i, ps=ps: e.activation(Et[i][:], ps[:, :], AF.Exp, scale=512.0 ** -0.5), [bps], [BEt[i]])
                            ets.append(i)
                        pz, bpz = PS[3], BPS[3]
                        mm_group(pz[:, :], bpz, [(ones_b[:], Et[i][:]) for i in ets], [Bones] + [BEt[i] for i in ets])
                        r_, br = rc[j % 2], Brc[j % 2]
                        p.op("dve", lambda e, r_=r_, pz=pz: e.reciprocal(r_[:], pz[:, :]), [bpz], [br])
                        for c in range(4):
                            po, bpo = PS[4 + c % 2], BPS[4 + c % 2]
                            col = (j * 4 + c) * 128
                            mm_group(po[:, :], bpo, [(vm[:, mbk, col:col + 128], Et[ets[mbk]][:]) for mbk in range(2)], [Bvm] + [BEt[i] for i in ets])
                            p.op("dve", lambda e, po=po, r_=r_, k=j * 4 + c: e.tensor_tensor(hT[:, k, ssl], po[:, :], r_[:], ALU.mult), [bpo, br], [BhT[s]])
                linear_T(ws, w_xo[l], hT, BhT, KC, blocksD, resid_evac(t0, BxT))

        def phase_G(st, l, S):
            p.reset_pool()
            NG = S // TG
            NHF = 4
            FH = DFF // NHF
            xs = sb(st, "g_xs", [128, KC, SB]); Bxs = Buf("g_xs")
            sq = sb(st, "g_sq", [128, KC, SB], BF16); Bsq = [Buf(f"g_sq{i}") for i in range(4)]
            rs = sb(st, "g_rs", [128, SB]); Brs = Buf("g_rs")
            hT = sb(st, "g_hT", [128, KC, TG], BF16); BhT = [Buf("g_hT0"), Buf("g_hT1")]
            uT = sb(st, "g_uT", [128, FH // 128, TG], BF16); BuT = [Buf("g_uT0"), Buf("g_uT1")]
            ws1 = WStream(st, "g_w1", KC)
            ws2 = WStream(st, "g_w2", FH // 128, nbuf=2, width=512)
            xr = [sb(st, f"g_xr{i}", [128, SB]) for i in range(4)]; Bxr = [Buf(f"g_xr{i}") for i in range(4)]
            rot = {"x": 0}
            for g in range(NG):
                t0 = g * TG
                BxT = [[Buf(f"xT{k}_{s_}") for s_ in range(2)] for k in range(KC)]
                norm_group(st, (xs, Bxs), hT, BhT, t0, lambda k: gcol_ffn(l, k), sq, Bsq, rs, Brs, BxT)
                for hf in range(NHF):
                    f0 = hf * FH
                    def evu(cc, m, tag, s, ps, bps):
                        k = (cc - f0) // 128
                        dst = uT[:, k, s * SB:(s + 1) * SB]
                        p.op("act", lambda e: e.activation(dst, ps[:, :], AF.Relu), [bps], [BuT[s]])
                        if k % 2 == 0:
                            p.op("dve", lambda e: e.tensor_tensor(dst, dst, dst, ALU.mult), [BuT[s]], [BuT[s]])
                        else:
                            p.op("act", lambda e: e.activation(dst, dst, AF.Square), [BuT[s]], [BuT[s]], self_ok=True)
                    blocks = [(f0 + cb * 512, f0 + (cb + 1) * 512, [(f0 + cb * 512 + i * 128, 128, "u") for i in range(4)]) for cb in range(FH // 512)]
                    linear_T(ws1, w_f1[l], hT, BhT, KC, blocks, evu)
                    def evy(cc, m, tag, s, ps, bps):
                        i = rot["x"] % 4; rot["x"] += 1
                        tsl = slice(t0 + s * SB, t0 + (s + 1) * SB)
                        bx = BxT[cc // 128][s]
                        p.dma("sp", xr[i][:], xT[cc:cc + 128, tsl], [bx], [Bxr[i]], Bxr[i])
                        p.op("dve", lambda e: e.tensor_tensor(xr[i][:], xr[i][:], ps[:, :], ALU.add), [Bxr[i], bps], [Bxr[i]])
                        p.dma("sp", xT[cc:cc + 128, tsl], xr[i][:], [Bxr[i]], [bx], Bxr[i])
                    blocks2 = [(cb * 512, (cb + 1) * 512, [(cb * 512 + i * 128, 128, "y") for i in range(4)]) for cb in range(4)]
                    linear_T(ws2, w_f2[l], uT, BuT, FH // 128, blocks2, evy, r0=f0)

        try:
            run_sequence("p", SP)
            run_sequence("s", SS)
        except _Stop:
            pass
        p.barrier()
        print("instructions ~", p.ninstr, flush=True)
    return nc


_CACHE = {}
DEBUG_STOP = None


def _consts(SMAX):
    half = 32
    pos = np.arange(SMAX, dtype=np.float32)
    inv_freq = (np.float32(10000.0) ** (-np.arange(half, dtype=np.float32) / np.float32(half))).astype(np.float32)
    ang = (pos[:, None] * inv_freq[None, :]).astype(np.float32)
    cos = np.cos(ang).astype(np.float32).T
    sin = np.sin(ang).astype(np.float32).T
    cos_t = np.ascontiguousarray(np.concatenate([cos, cos], 0))
    sin_t = np.ascontiguousarray(np.concatenate([-sin, sin], 0))
    ident = np.eye(128, dtype=np.float32)
    masks = np.zeros((128, 3, 512), np.float32)
    s_idx = (np.arange(128) % 64)[:, None]
    t_idx = np.arange(64)[None, :]
    masks[:, 0, 0:64] = (s_idx <= t_idx)
    masks[:, 1, 0:64] = (s_idx >= t_idx)
    masks[:, 2, :] = 1.0
    masks[:, 2, 0::64] = 0.0
    return cos_t, sin_t, ident, masks


def run(inputs, SP, SS, ncores=2):
    key = (SP, SS)
    if key not in _CACHE:
        _CACHE[key] = build_program(SP, SS)
    nc = _CACHE[key]
    f = lambda a: np.ascontiguousarray(np.asarray(a, dtype=np.float32))
    cos_t, sin_t, ident, masks = _consts(max(SP, SS))
    prm_a = np.zeros((128, 128), np.float32)
    prm_a[0:32] = f(inputs["mix_norm"]).reshape(32, 128)
    prm_a[32:40] = f(inputs["q_norm"]).reshape(8, 128)
    prm_a[40:44] = f(inputs["kv_norm"]).reshape(4, 128)
    prm_a[44:76] = f(inputs["hgrn_lb_logits"]).reshape(32, 128)
    prm_a[76:78] = f(inputs["hgrn_out_norm"]).reshape(2, 128)
    prm_b = np.zeros((128, 128), np.float32)
    prm_b[0:32] = f(inputs["xattn_norm"]).reshape(32, 128)
    prm_b[32:64] = f(inputs["mem_norm"]).reshape(32, 128)
    prm_b[64:96] = f(inputs["ffn_norm"]).reshape(32, 128)
    shared = {k: f(inputs[k]) for k in ("w_in", "w_uq", "w_ukv", "w_out", "w_xq", "w_xk", "w_xv", "w_xo", "w_ffn1", "w_ffn2", "final_norm")}
    shared.update(prm_a=prm_a, prm_b=prm_b, cos_t=cos_t, sin_t=sin_t, ident=ident, masks=masks)
    xp = f(inputs["x_prompt"]); xs = f(inputs["x_sample"]); mp = f(inputs["mem_prompt"]); ms = f(inputs["mem_sample"])
    in_maps = []
    for c in range(ncores):
        m = dict(shared)
        m.update(x_p=xp[c], x_s=xs[c], mem_p=mp[c], mem_s=ms[c])
        in_maps.append(m)
    res = run_bass_kernel_spmd(nc, in_maps, core_ids=list(range(ncores)))
    yp = np.stack([np.asarray(res.results[c]["y_p"], dtype=np.float32) for c in range(ncores)], 0)
    ys = np.stack([np.asarray(res.results[c]["y_s"], dtype=np.float32) for c in range(ncores)], 0)
    return yp, ys


def kernel(**inputs):
    SP = int(np.asarray(inputs["x_prompt"]).shape[1])
    SS = int(np.asarray(inputs["x_sample"]).shape[1])
    return run(inputs, SP, SS, ncores=2)
```
